# Optimizing a Trainium2 kernel written in Bass

```python
import math
import jax, jax.numpy as jnp
from jax import lax
import numpy as np

D_MODEL = 1024
BATCH = 8
SEQ = 2048
DEPTH = 1
DEC_BATCH = 128
DEC_SEQ = 4
PAST_LEN = 8192
PAGE_SIZE = 128

N_HEADS = 8
KV_HEADS = 2
GROUP = N_HEADS // KV_HEADS
HEAD_DIM = 64
Q_W = N_HEADS * HEAD_DIM
KV_W = KV_HEADS * HEAD_DIM
WINDOW = 128
REL_BUCKETS = 32
REL_MAX_DIST = 128
D_RNN = D_MODEL
RNN_BLOCKS = 16
RNN_BS = D_RNN // RNN_BLOCKS
CONV_W = 4
RG_C = 8.0
D_FF = 4 * D_MODEL
PLE_DIM = 256
EPS = 1e-6
NEG_INF = -1e30
IN_SIZES = (Q_W, KV_W, KV_W, D_RNN, D_RNN, D_MODEL, D_MODEL)
IN_COLS = sum(IN_SIZES)
IN_SPLITS = [int(s) for s in np.cumsum(IN_SIZES)[:-1]]

kernel_name = "hybrid_swa_sink_rglru_decode_step"


def rmsnorm(x, g):
    xf = x.astype(jnp.float32)
    y = xf * lax.rsqrt(jnp.mean(xf * xf, axis=-1, keepdims=True) + EPS) * g.astype(jnp.float32)
    return y.astype(x.dtype)


def rel_bucket(dist):
    n = jnp.maximum(dist, 0)
    max_exact = REL_BUCKETS // 2
    nf = jnp.maximum(n, 1).astype(jnp.float32)
    large = max_exact + (jnp.log(nf / max_exact) / math.log(REL_MAX_DIST / max_exact)
                         * (REL_BUCKETS - max_exact)).astype(jnp.int32)
    large = jnp.minimum(large, REL_BUCKETS - 1)
    return jnp.where(n < max_exact, n, large)


def window_attend(q, k, v, key_valid, sinks, rel_bias):
    B, N, Tq, H, Dh = q.shape
    Tk = k.shape[2]
    qg = q.reshape(B, N, Tq, KV_HEADS, GROUP, Dh)
    logits = jnp.einsum('bnqkgd,bnskd->bnkgqs', qg, k).astype(jnp.float32) * (Dh ** -0.5)
    dist = jnp.arange(Tq)[:, None] + (Tk - Tq) - jnp.arange(Tk)[None, :]
    bias = rel_bias[rel_bucket(dist)].astype(jnp.float32)
    bias = jnp.transpose(bias, (2, 0, 1)).reshape(KV_HEADS, GROUP, Tq, Tk)
    mask = (dist >= 0) & (dist <= WINDOW)
    mask = mask & key_valid[:, None, None, None, :]
    logits = jnp.where(mask, logits + bias, NEG_INF)
    sink = sinks.astype(jnp.float32).reshape(KV_HEADS, GROUP, 1, 1)
    m = jnp.maximum(jnp.max(logits, axis=-1, keepdims=True), sink)
    e = jnp.exp(logits - m)
    probs = e / (jnp.sum(e, axis=-1, keepdims=True) + jnp.exp(sink - m))
    out = jnp.einsum('bnkgqs,bnskd->bnqkgd', probs.astype(v.dtype), v)
    return out.reshape(B, N, Tq, H * Dh)


def causal_conv(x, prev, w, b):
    T = x.shape[1]
    xp = jnp.concatenate([prev.astype(x.dtype), x], axis=1)
    y = b + sum(w[j] * xp[:, j:j + T] for j in range(CONV_W))
    return y, xp[:, -(CONV_W - 1):]


def rglru(x, h0, wa, ba, wx, bx, lam):
    B, T, _ = x.shape
    xb = x.reshape(B, T, RNN_BLOCKS, RNN_BS)
    r = jax.nn.sigmoid((jnp.einsum('btnd,nde->btne', xb, wa).reshape(B, T, D_RNN) + ba).astype(jnp.float32))
    i = jax.nn.sigmoid((jnp.einsum('btnd,nde->btne', xb, wx).reshape(B, T, D_RNN) + bx).astype(jnp.float32))
    log_a = -RG_C * jax.nn.softplus(-lam.astype(jnp.float32)) * r
    a = jnp.exp(log_a)
    bterm = jnp.sqrt(-jnp.expm1(2.0 * log_a)) * (i * x.astype(jnp.float32))
    bterm = bterm.at[:, 0].add(a[:, 0] * h0.astype(jnp.float32))

    def combine(c1, c2):
        a1, b1 = c1
        a2, b2 = c2
        return a1 * a2, a2 * b1 + b2

    _, h = lax.associative_scan(combine, (a, bterm), axis=1)
    return h, h[:, -1]


def layer(x, p, k_past, v_past, conv_prev, h0, rel_bias, lw):
    B, T, _ = x.shape
    xn = rmsnorm(x, lw['norm1_g'])
    proj = xn @ lw['w_in']
    q, k, v, xr, gr, ga_logit, gr_logit = jnp.split(proj, IN_SPLITS, axis=-1)
    q = rmsnorm(q.reshape(B, T, N_HEADS, HEAD_DIM), lw['q_norm_g'])
    k = rmsnorm(k.reshape(B, T, KV_HEADS, HEAD_DIM), lw['k_norm_g'])
    v = v.reshape(B, T, KV_HEADS, HEAD_DIM)
    if k_past is None:
        nb = T // WINDOW
        qb = q.reshape(B, nb, WINDOW, N_HEADS, HEAD_DIM)
        kb = k.reshape(B, nb, WINDOW, KV_HEADS, HEAD_DIM)
        vb = v.reshape(B, nb, WINDOW, KV_HEADS, HEAD_DIM)
        kk = jnp.concatenate([jnp.concatenate([jnp.zeros_like(kb[:, :1]), kb[:, :-1]], axis=1), kb], axis=2)
        vv = jnp.concatenate([jnp.concatenate([jnp.zeros_like(vb[:, :1]), vb[:, :-1]], axis=1), vb], axis=2)
        key_valid = (jnp.arange(nb)[:, None] > 0) | (jnp.arange(2 * WINDOW)[None, :] >= WINDOW)
        att = window_attend(qb, kk, vv, key_valid, lw['sinks'], rel_bias).reshape(B, T, Q_W)
        k_all, v_all = k, v
        conv_prev = jnp.zeros((B, CONV_W - 1, D_RNN), x.dtype)
        h0 = jnp.zeros((B, D_RNN), jnp.float32)
    else:
        k_all = jnp.concatenate([k_past.astype(k.dtype), k], axis=1)
        v_all = jnp.concatenate([v_past.astype(v.dtype), v], axis=1)
        key_valid = jnp.ones((1, k_all.shape[1]), bool)
        att = window_attend(q[:, None], k_all[:, None], v_all[:, None], key_valid,
                            lw['sinks'], rel_bias)[:, 0]
    new_k = k_all[:, -WINDOW:]
    new_v = v_all[:, -WINDOW:]
    xc, new_conv = causal_conv(xr, conv_prev, lw['conv_w'], lw['conv_b'])
    h, h_last = rglru(xc, h0, lw['rg_wa'], lw['rg_ba'], lw['rg_wx'], lw['rg_bx'], lw['rg_lambda'])
    rnn = (h.astype(x.dtype) * jax.nn.gelu(gr)) @ lw['w_o_rnn']
    mix = (jax.nn.sigmoid(ga_logit) * (att @ lw['w_o_attn']) + jax.nn.sigmoid(gr_logit) * rnn) @ lw['w_out']
    x = x + mix
    hmid = jax.nn.relu(rmsnorm(x, lw['norm2_g']) @ lw['w_up'])
    x = x + (hmid * hmid) @ lw['w_down']
    gate = jax.nn.sigmoid(rmsnorm(x, lw['ple_norm_g']) @ lw['w_ple_gate'])
    x = x + gate * (p.astype(x.dtype) @ lw['w_ple'])
    return x, new_k, new_v, new_conv, h_last


def setup_inputs(seed: int = 0) -> dict:
    key = jax.random.key(seed)
    ks = jax.random.split(key, 32)
    f32 = jnp.float32

    def nrm(k, shape, scale=1.0):
        return jax.random.normal(k, shape, f32) * scale

    def gain(k, n):
        return 1.0 + 0.02 * jax.random.normal(k, (DEPTH, n), f32)

    u = jax.random.uniform(ks[20], (DEPTH, D_RNN), f32, minval=0.9, maxval=0.999)
    a0 = u ** (1.0 / RG_C)
    return {
        'x_prompt': nrm(ks[0], (BATCH, SEQ, D_MODEL)),
        'x_sample': nrm(ks[1], (DEC_BATCH, DEC_SEQ, D_MODEL)),
        'cache_k_win': nrm(ks[2], (DEPTH, DEC_BATCH, WINDOW, KV_HEADS, HEAD_DIM)),
        'cache_v_win': nrm(ks[3], (DEPTH, DEC_BATCH, WINDOW, KV_HEADS, HEAD_DIM)),
        'state_conv': nrm(ks[4], (DEPTH, DEC_BATCH, CONV_W - 1, D_RNN)),
        'state_h': nrm(ks[5], (DEPTH, DEC_BATCH, D_RNN), 0.5),
        'p_prompt': nrm(ks[6], (DEPTH, BATCH, SEQ, PLE_DIM)),
        'p_sample': nrm(ks[7], (DEPTH, DEC_BATCH, DEC_SEQ, PLE_DIM)),
        'rel_bias': nrm(ks[8], (REL_BUCKETS, N_HEADS), 0.2),
        'norm1_g': gain(ks[9], D_MODEL),
        'w_in': nrm(ks[10], (DEPTH, D_MODEL, IN_COLS), D_MODEL ** -0.5),
        'q_norm_g': gain(ks[11], HEAD_DIM),
        'k_norm_g': gain(ks[12], HEAD_DIM),
        'sinks': nrm(ks[13], (DEPTH, N_HEADS), 0.5),
        'w_o_attn': nrm(ks[14], (DEPTH, Q_W, D_MODEL), Q_W ** -0.5),
        'conv_w': nrm(ks[15], (DEPTH, CONV_W, D_RNN), CONV_W ** -0.5),
        'conv_b': nrm(ks[16], (DEPTH, D_RNN), 0.02),
        'rg_wa': nrm(ks[17], (DEPTH, RNN_BLOCKS, RNN_BS, RNN_BS), RNN_BS ** -0.5),
        'rg_ba': nrm(ks[18], (DEPTH, D_RNN), 0.02),
        'rg_wx': nrm(ks[19], (DEPTH, RNN_BLOCKS, RNN_BS, RNN_BS), RNN_BS ** -0.5),
        'rg_bx': nrm(ks[21], (DEPTH, D_RNN), 0.02),
        'rg_lambda': jnp.log(a0) - jnp.log1p(-a0),
        'w_o_rnn': nrm(ks[22], (DEPTH, D_RNN, D_MODEL), D_RNN ** -0.5),
        'w_out': nrm(ks[23], (DEPTH, D_MODEL, D_MODEL), D_MODEL ** -0.5),
        'norm2_g': gain(ks[24], D_MODEL),
        'w_up': nrm(ks[25], (DEPTH, D_MODEL, D_FF), D_MODEL ** -0.5),
        'w_down': nrm(ks[26], (DEPTH, D_FF, D_MODEL), D_FF ** -0.5),
        'ple_norm_g': gain(ks[27], D_MODEL),
        'w_ple_gate': nrm(ks[28], (DEPTH, D_MODEL, D_MODEL), D_MODEL ** -0.5),
        'w_ple': nrm(ks[29], (DEPTH, PLE_DIM, D_MODEL), PLE_DIM ** -0.5),
    }


def reference(x_prompt, x_sample, cache_k_win, cache_v_win, state_conv, state_h, p_prompt, p_sample,
              rel_bias, norm1_g, w_in, q_norm_g, k_norm_g, sinks, w_o_attn, conv_w, conv_b,
              rg_wa, rg_ba, rg_wx, rg_bx, rg_lambda, w_o_rnn, w_out, norm2_g, w_up, w_down,
              ple_norm_g, w_ple_gate, w_ple):
    xp, xs = x_prompt, x_sample
    kp_l, vp_l, cp_l, hp_l, ksl, vsl, csl, hsl = [], [], [], [], [], [], [], []
    for i in range(DEPTH):
        lw = dict(norm1_g=norm1_g[i], w_in=w_in[i], q_norm_g=q_norm_g[i], k_norm_g=k_norm_g[i],
                  sinks=sinks[i], w_o_attn=w_o_attn[i], conv_w=conv_w[i], conv_b=conv_b[i],
                  rg_wa=rg_wa[i], rg_ba=rg_ba[i], rg_wx=rg_wx[i], rg_bx=rg_bx[i],
                  rg_lambda=rg_lambda[i], w_o_rnn=w_o_rnn[i], w_out=w_out[i], norm2_g=norm2_g[i],
                  w_up=w_up[i], w_down=w_down[i], ple_norm_g=ple_norm_g[i],
                  w_ple_gate=w_ple_gate[i], w_ple=w_ple[i])
        xp, kp, vp, cp, hp = layer(xp, p_prompt[i], None, None, None, None, rel_bias, lw)
        xs, kss, vss, css, hss = layer(xs, p_sample[i], cache_k_win[i], cache_v_win[i],
                                       state_conv[i], state_h[i], rel_bias, lw)
        kp_l.append(kp); vp_l.append(vp); cp_l.append(cp); hp_l.append(hp)
        ksl.append(kss); vsl.append(vss); csl.append(css); hsl.append(hss)
    return (xp, xs,
            jnp.stack(kp_l), jnp.stack(vp_l), jnp.stack(cp_l), jnp.stack(hp_l),
            jnp.stack(ksl), jnp.stack(vsl), jnp.stack(csl), jnp.stack(hsl))
```

```python
import math
import os
from contextlib import ExitStack

import numpy as np
import concourse.bass as bass
import concourse.mybir as mybir
from concourse.bass_utils import run_bass_kernel_spmd

F32 = mybir.dt.float32
BF16 = mybir.dt.bfloat16
AF = mybir.ActivationFunctionType
ALU = mybir.AluOpType

D = 1024
SEQ = 2048
NT = 4
DEC_B = 16
DEC_T = 4
NS_TOK = DEC_B * DEC_T
IN_COLS = 4864
EPS = 1e-6
ENG_NAMES = ("pe", "act", "dve", "pool", "sp")
_KT_SAMPLE = int(os.environ.get("KT_SAMPLE", "1"))
NSLOT = 4
EPOCH = 400
N_EPOCH = 6
PF_DEPTH = 2

C_G1, C_G2, C_G3, C_CW, C_CB, C_BA, C_BX, C_LAM, C_QG, C_KG, C_SINK, NSMALL = 0, 8, 16, 24, 56, 64, 72, 80, 88, 89, 90, 98
D_HBA, D_HBX, D_CC, D_HC, D_ES, D_KG8, D_TMP, NDER = 0, 8, 16, 24, 32, 40, 41, 64


class Sched:
    def __init__(self, n_dma_sems=48):
        self.ops = {e: [] for e in ENG_NAMES}
        self.count = {e: 0 for e in ENG_NAMES}
        self.seen = {e: {} for e in ENG_NAMES}
        self.last_write = {}
        self.readers = {}
        self.n_dma_sems = n_dma_sems
        self.dma_next = 0
        self.dma_next_sw = 0
        self.dma_val = [0] * n_dma_sems

    def _collect(self, eng, reads, writes):
        deps = []
        for r in reads:
            d = self.last_write.get(r)
            if d is not None:
                deps.append((d, "raw"))
        for w in writes:
            d = self.last_write.get(w)
            if d is not None:
                deps.append((d, "waw"))
            for d in self.readers.get(w, ()):
                deps.append((d, "war"))
        need = {}
        for (src, val), kind in deps:
            if src == eng:
                if eng == "pe":
                    continue
            if val <= self.seen[eng].get(src, 0):
                continue
            if val > need.get(src, 0):
                need[src] = val
        return need

    def _emit_waits(self, eng, need):
        for src, val in need.items():
            if not isinstance(src, tuple):
                assert val <= self.count[src], ("open ticket", eng, src, val, self.count[src])
            self.seen[eng][src] = val
            self.ops[eng].append(("wait", src, val))

    def _record(self, dep, reads, writes):
        for r in reads:
            lst = self.readers.setdefault(r, [])
            lst[:] = [d for d in lst if d[0] != dep[0]]
            lst.append(dep)
        for w in writes:
            self.last_write[w] = dep
            self.readers[w] = []

    def op(self, eng, fn, reads=(), writes=(), inc=True):
        need = self._collect(eng, reads, writes)
        self._emit_waits(eng, need)
        if inc:
            self.count[eng] += 1
            ticket = self.count[eng]
        else:
            ticket = self.count[eng] + 1
        self.ops[eng].append(("op", fn, inc, ticket))
        self._record((eng, ticket), reads, writes)

    def dma(self, eng, out, in_, reads=(), writes=(), **kw):
        need = self._collect(eng, reads, writes)
        n_sw = self.n_dma_sems // 3
        if eng == "pool":
            idx = self.dma_next_sw
            self.dma_next_sw = (self.dma_next_sw + 1) % n_sw
        else:
            idx = n_sw + self.dma_next
            self.dma_next = (self.dma_next + 1) % (self.n_dma_sems - n_sw)
        prev = self.dma_val[idx]
        src = ("dma", idx)
        if prev > self.seen[eng].get(src, 0):
            need[src] = max(need.get(src, 0), prev)
        self._emit_waits(eng, need)
        self.dma_val[idx] = prev + 16
        self.ops[eng].append(("dma", out, in_, idx, kw))
        self._record((src, prev + 16), reads, writes)

    def sync_all(self, engs=ENG_NAMES):
        for eng in engs:
            need = {}
            for idx in range(self.n_dma_sems):
                v = self.dma_val[idx]
                if v > self.seen[eng].get(("dma", idx), 0):
                    need[("dma", idx)] = v
            for e in ENG_NAMES:
                if e != eng and self.count[e] > self.seen[eng].get(e, 0):
                    need[e] = self.count[e]
            self._emit_waits(eng, need)

    def run(self, block, sems, dma_sems):
        handles = {"pe": "tensor", "act": "scalar", "dve": "vector", "pool": "gpsimd", "sp": "sync"}

        def make(engname):
            def body(engine):
                for item in self.ops[engname]:
                    if item[0] == "wait":
                        _, src, val = item
                        if isinstance(src, tuple):
                            engine.wait_ge(dma_sems[src[1]], val)
                        else:
                            engine.wait_ge(sems[src][(val - 1) // EPOCH], (val - 1) % EPOCH + 1)
                    elif item[0] == "op":
                        _, fn, inc, ticket = item
                        ins = fn(engine)
                        if inc:
                            ins.then_inc(sems[engname][(ticket - 1) // EPOCH], 1)
                    else:
                        _, out, in_, idx, kw = item
                        engine.dma_start(out=out, in_=in_, **kw).then_inc(dma_sems[idx], 16)
            return body

        for engname in ENG_NAMES:
            getattr(block, handles[engname])(make(engname))


def I(name, *args, **kw):
    return lambda e: getattr(e, name)(*args, **kw)


def rel_bucket_np(dist):
    n = np.maximum(dist, 0)
    max_exact = 16
    nf = np.maximum(n, 1).astype(np.float32)
    large = max_exact + (np.log(nf / np.float32(max_exact)) / np.float32(math.log(128 / max_exact))
                         * np.float32(32 - max_exact)).astype(np.int32)
    large = np.minimum(large, 31)
    return np.where(n < max_exact, n, large)


CHUNKS = ([x for c in range(8) for x in (("xr", c), ("grl", c))] + [("q", c) for c in range(4)] + [("k", 0), ("v", 0)]
          + [("gr", c) for c in range(8)] + [("ga", c) for c in range(8)])
ATT_AFTER_G = {5: 0, 6: 1, 7: 2, 8: 3}
P2_SCHED = {("grl", 3): ([0, 1], "a"), ("grl", 4): ([0, 1], "b"), ("grl", 5): ([2, 3], "a"), ("grl", 6): ([2, 3], "b"),
            ("grl", 7): ([4, 5], "a"), ("q", 0): ([4, 5], "b"), ("q", 1): ([6, 7], "a"), ("q", 2): ([6, 7], "b")}


def piece_list():
    P = []
    for g in range(10):
        P.append(("in%d" % g, "w_in", 0, 8, g * 512, 512 if g < 9 else 256))
    P.append(("oa", "w_oa", 0, 4, 0, 1024))
    P.append(("or0", "w_or", 0, 8, 0, 512))
    P.append(("or1", "w_or", 0, 8, 512, 512))
    P.append(("out0", "w_out", 0, 8, 0, 512))
    P.append(("out1", "w_out", 0, 8, 512, 512))
    for g in range(8):
        P.append(("up%d" % g, "w_up", 0, 8, g * 512, 512))
    for g in range(8):
        P.append(("dn%d" % g, "w_down", 4 * g, 4, 0, 1024))
    P.append(("pg0", "w_pg", 0, 8, 0, 512))
    P.append(("pg1", "w_pg", 0, 8, 512, 512))
    return P


def build_nc():
    nc = bass.Bass("TRN2", target_bir_lowering=False)
    S = Sched()

    def din(name, shape, dt=F32):
        return nc.dram_tensor(name, list(shape), dt, kind="ExternalInput").ap()

    def dout(name, shape):
        return nc.dram_tensor(name, list(shape), F32, kind="ExternalOutput").ap()

    xp = din("xp", [SEQ, D]); xs = din("xs", [DEC_B, DEC_T, D])
    pp = din("pp", [SEQ, 256]); psm = din("psm", [DEC_B, DEC_T, 256])
    ck = din("ck", [DEC_B, 128, 128]); cv = din("cv", [DEC_B, 128, 128])
    sconv = din("sconv", [DEC_B, 3, D]); sh = din("sh", [DEC_B, D])
    wd = {"w_in": din("w_in", [D, IN_COLS]), "w_oa": din("w_oa", [512, D]), "w_or": din("w_or", [D, D]),
          "w_out": din("w_out", [D, D]), "w_up": din("w_up", [D, 4096]), "w_down": din("w_down", [4096, D]),
          "w_pg": din("w_pg", [D, D]), "w_ple": din("w_ple", [256, D])}
    rga_d = din("rga", [128, 8, 128]); rgx_d = din("rgx", [128, 8, 128])
    small_d = din("small", [128, NSMALL]); relb_d = din("relb", [32, 8]); oh_d = din("oh", [33, 384])

    y_p = dout("y_p", [SEQ, D]); y_s = dout("y_s", [DEC_B, DEC_T, D])
    kwp = dout("kwp", [128, 128]); vwp = dout("vwp", [128, 128])
    convp = dout("convp", [3, D]); hp = dout("hp", [1, D])
    kws = dout("kws", [DEC_B, 128, 128]); vws = dout("vws", [DEC_B, 128, 128])
    convs = dout("convs", [DEC_B, 3, D]); hs = dout("hs", [DEC_B, D])

    pieces = piece_list()
    scr = {}
    for key, wn, r0, nkc, c0, ncol in pieces + [("ple", "w_ple", 0, 2, 0, 1024)]:
        scr[key] = nc.dram_tensor("scr_" + key, [128, nkc * ncol], BF16, kind="Internal").ap()
    eg_scr = nc.dram_tensor("eg_scr", [8, 384], F32, kind="Internal").ap()
    vn_scr = nc.dram_tensor("vn_scr", [NS_TOK, 128], F32, kind="Internal").ap()

    with ExitStack() as es:
        def sb(name, shape, dt):
            return es.enter_context(nc.sbuf_tensor("sb_" + name, list(shape), dt))

        identb = sb("identb", [128, 128], BF16)
        identf = sb("identf", [128, 128], F32)
        Jf = sb("Jf", [128, 128], F32)
        onesblk = sb("onesblk", [128, 128], BF16)
        ones128 = sb("ones128", [128, 128], BF16)
        small = sb("small", [128, NSMALL], F32)
        der = sb("der", [128, NDER], F32)
        eps64 = sb("eps64", [128, 1], F32)
        ET = sb("ET", [128, 2, 8, 128], F32)
        rgA = sb("rgA", [128, 8, 128], BF16)
        rgX = sb("rgX", [128, 8, 128], BF16)
        x_tm = sb("x_tm", [128, 4, D], F32)
        XG = sb("XG", [128, 8, 512], BF16)
        xT = sb("xT", [128, 8, 512], BF16)
        p_bf = sb("p_bf", [128, 4, 256], BF16)
        pT = sb("pT", [128, 2, 512], BF16)
        qT = sb("qT", [128, 4, 512], BF16)
        kT = sb("kT", [128, 640], BF16)
        kTf = sb("kTf", [128, 128], F32)
        V = sb("V", [128, 5, 2, 65], BF16)
        xr_flat = sb("xr", [128, 8 * 515], F32)
        xr = xr_flat[:, :].rearrange("p (c n) -> p c n", n=515)
        XCR = sb("XCR", [128, 4, 512], F32)
        XCBR = sb("XCBR", [128, 4, 512], BF16)
        P2 = sb("P2", [128, 3, 2, 512], F32)
        xrS = sb("xrS", [128, 8, 112], F32)
        Wple_sb = sb("Wple_sb", [128, 2, 1024], BF16)
        sqbuf = sb("sqbuf", [128, 512], BF16)
        PTA = sb("PTA", [128, 512], BF16)
        PTB = sb("PTB", [4, 512], BF16)
        SINKC = sb("SINKC", [128, 512], F32)
        HS0 = sb("HS0", [128, 8, DEC_B], F32)
        HSout = sb("HSout", [128, 8, DEC_B], F32)
        convst = sb("convst", [128, 8, 4], F32)
        kTs = sb("kTs", [128, NS_TOK], BF16)
        RIN = sb("RIN", [128, 8, 512], BF16)
        SH = sb("SH", [128, 16, 512], F32)
        attT = sb("attT", [128, 4, 512], BF16)
        att_tm = [sb("att_tm%d" % i, [128, 512], BF16) for i in range(2)]
        SCR = [sb("scr%d" % i, [128, 512], F32) for i in range(6)]
        PT = [sb("PT%d" % i, [128, 2, 4, 128], BF16) for i in range(2)]
        stats = sb("stats", [128, 64], F32)
        hstate = sb("hstate", [128, 8], F32)
        slots = [sb("slot%d" % i, [128, 4096], BF16) for i in range(NSLOT)]
        PS = es.enter_context(nc.psum_tensor("PS", [128, 8, 512], F32))
        sems = {e: [es.enter_context(nc.semaphore("s_%s_%d" % (e, i))) for i in range(N_EPOCH)] for e in ENG_NAMES}
        dsems = [es.enter_context(nc.semaphore("d%d" % i)) for i in range(S.n_dma_sems)]
        block = es.enter_context(nc.Block())

        hmid = SH[:, :, :].bitcast(BF16)

        def hm(k, c0, c1):
            return hmid[:, k // 2, (k % 2) * 512 + c0:(k % 2) * 512 + c1]

        def r_sh(cells):
            return ["SH%d" % c for c in cells]

        def psb(bank):
            return PS[:, bank, :].bitcast(BF16)

        main_rr = [0]
        sample_hooks = {}
        fm_srcs_conv, fm_srcs_h = [], []

        def next_main():
            b = main_rr[0]
            main_rr[0] = (b + 1) % 4
            return b

        for tb in range(4):
            S.dma("sp", x_tm[:, tb, :], xp[tb * 128:(tb + 1) * 128, :], writes=["x%d" % tb])
        S.dma("sp", small[:], small_d, writes=["small"])
        S.dma("pool", rgA[:], rga_d, writes=["rgA"])
        S.dma("pool", rgX[:], rgx_d, writes=["rgX"])
        S.op("pool", I("memset", identf[:], 0.0), writes=["identf"])
        S.op("pool", I("affine_select", out=identf[:], in_=identf[:], pattern=[[-1, 128]], compare_op=ALU.not_equal,
                       fill=1.0, base=0, channel_multiplier=1), reads=["identf"], writes=["identf"])
        S.op("pool", I("memset", Jf[:], 0.0), writes=["Jf"])
        S.op("pool", I("affine_select", out=Jf[:], in_=Jf[:], pattern=[[1, 128]], compare_op=ALU.not_equal,
                       fill=1.0, base=-127, channel_multiplier=1), reads=["Jf"], writes=["Jf"])
        S.op("dve", I("tensor_copy", out=identb[:], in_=identf[:]), reads=["identf"], writes=["identb"])
        S.op("pool", I("memset", onesblk[:], 0.0), writes=["onesblk"])
        S.op("pool", I("memset", onesblk[0:64, 0:64], 1.0), writes=["onesblk"])
        S.op("pool", I("memset", onesblk[64:128, 64:128], 1.0), writes=["onesblk"])
        S.op("pool", I("memset", ones128[:], 1.0), writes=["ones128"])
        S.op("pool", I("memset", eps64[:], 64.0 * EPS), writes=["eps64"])
        S.op("pool", I("memset", hstate[:], 0.0), writes=["hstate"])
        S.op("pool", I("memset", convst[:], 0.0), writes=["convst"])
        S.op("pool", I("memset", xr[:, :, 0:3], 0.0), writes=["xrh%d" % c for c in range(8)])
        S.op("pool", I("memset", V[:], 1.0), writes=["V%d" % i for i in range(5)])
        S.op("pool", I("memset", stats[:], 0.0), writes=["stats"] + ["st%d" % i for i in range(12)] + ["den0", "den1"])
        S.op("dve", I("tensor_scalar", out=der[:, D_HBA:D_HBA + 16], in0=small[:, C_BA:C_BA + 16], scalar1=0.5, scalar2=None,
                      op0=ALU.mult), reads=["small"], writes=["der_a"])
        S.op("act", I("activation", out=der[:, D_TMP:D_TMP + 8], in_=small[:, C_LAM:C_LAM + 8], func=AF.Exp, scale=-1.0),
             reads=["small"], writes=["der_t"])
        S.op("act", I("activation", out=der[:, D_TMP:D_TMP + 8], in_=der[:, D_TMP:D_TMP + 8], func=AF.Ln, bias=1.0),
             reads=["der_t"], writes=["der_t"])
        S.op("dve", I("tensor_scalar", out=der[:, D_CC:D_CC + 8], in0=der[:, D_TMP:D_TMP + 8], scalar1=-8.0, scalar2=None,
                      op0=ALU.mult), reads=["der_t"], writes=["der_c"])
        S.op("dve", I("tensor_scalar", out=der[:, D_HC:D_HC + 8], in0=der[:, D_TMP:D_TMP + 8], scalar1=-4.0, scalar2=None,
                      op0=ALU.mult), reads=["der_t"], writes=["der_c2"])
        S.op("act", I("activation", out=der[:, D_ES:D_ES + 8], in_=small[:, C_SINK:C_SINK + 8], func=AF.Exp),
             reads=["small"], writes=["der_e"])
        S.op("dve", I("tensor_scalar", out=der[:, D_KG8:D_KG8 + 1], in0=small[:, C_KG:C_KG + 1], scalar1=8.0, scalar2=None,
                      op0=ALU.mult), reads=["small"], writes=["der_k"])
        DER_ALL = ["der_a", "der_c", "der_c2", "der_e", "der_k", "small"]

        cast_list = [("ple", "w_ple", 0, 2, 0, 1024)] + pieces
        cast_state = {"n": 0}

        def ensure_cast(upto):
            while cast_state["n"] < min(upto, len(cast_list)):
                key, wn, r0, nkc, c0, ncol = cast_list[cast_state["n"]]
                cast_state["n"] += 1
                wv = wd[wn].rearrange("(kc p) n -> p kc n", p=128)
                sv = scr[key].rearrange("p (kc n) -> p kc n", n=ncol)
                hk = nkc // 2
                for h2 in range(2):
                    S.dma("pool", sv[:, h2 * hk:(h2 + 1) * hk, :], wv[:, r0 + h2 * hk:r0 + (h2 + 1) * hk, c0:c0 + ncol],
                          writes=["scrw_%s_%d" % (key, h2)])

        ensure_cast(4)

        tiles = list(range(NT)) + ["s"]
        stream = [(t, pi) for t in tiles for pi in range(len(pieces))]
        state = {"loaded": 0, "cursor": 0}

        def emit_load(i):
            t, pi = stream[i]
            key, wn, r0, nkc, c0, ncol = pieces[pi]
            ensure_cast(pi + 1 + 3)
            sl = i % NSLOT
            S.dma("sp", slots[sl][:, 0:nkc * ncol], scr[key], reads=["scrw_%s_0" % key, "scrw_%s_1" % key],
                  writes=["slot%d" % sl])

        def get_piece(key_expected):
            i = state["cursor"]
            t, pi = stream[i]
            assert pieces[pi][0] == key_expected, (pieces[pi][0], key_expected)
            while state["loaded"] < min(i + PF_DEPTH, len(stream)):
                emit_load(state["loaded"])
                state["loaded"] += 1
            state["cursor"] += 1
            sl = i % NSLOT
            key, wn, r0, nkc, c0, ncol = pieces[pi]
            return slots[sl][:, 0:nkc * ncol].rearrange("p (kc n) -> p kc n", n=ncol), "slot%d" % sl

        def rmsnorm_to_xT(N, NB, TB, gcol, statbase):
            st_all = ["st%d" % (statbase + tb) for tb in range(NB)]
            for tb in range(NB):
                xin = x_tm[0:TB, tb, :]
                xn = XG[0:TB, 2 * tb:2 * tb + 2, :].rearrange("p a b -> p (a b)")
                sc = stats[0:TB, statbase + tb:statbase + tb + 1]
                S.op("act", I("activation", out=xn, in_=xin, func=AF.Square, accum_out=sc),
                     reads=["x%d" % tb], writes=["XG%d" % (2 * tb), "XG%d" % (2 * tb + 1), "st%d" % (statbase + tb)])
            sca = stats[0:TB, statbase:statbase + NB]
            S.op("dve", I("tensor_scalar", out=sca, in0=sca, scalar1=1.0 / D, scalar2=EPS, op0=ALU.mult, op1=ALU.add),
                 reads=st_all, writes=st_all)
            S.op("act", I("activation", out=sca, in_=sca, func=AF.Sqrt), reads=st_all, writes=st_all)
            S.op("dve", I("reciprocal", out=sca, in_=sca), reads=st_all, writes=st_all)
            for tb in range(NB):
                xin = x_tm[0:TB, tb, :]
                xn = XG[0:TB, 2 * tb:2 * tb + 2, :].rearrange("p a b -> p (a b)")
                sc = stats[0:TB, statbase + tb:statbase + tb + 1]
                if tb % 2 == 0:
                    S.op("dve", I("tensor_scalar", out=xn, in0=xin, scalar1=sc, scalar2=None, op0=ALU.mult),
                         reads=["x%d" % tb, "st%d" % (statbase + tb)], writes=["XG%d" % (2 * tb), "XG%d" % (2 * tb + 1)])
                else:
                    S.op("act", I("activation", out=xn, in_=xin, func=AF.Copy, scale=sc),
                         reads=["x%d" % tb, "st%d" % (statbase + tb)], writes=["XG%d" % (2 * tb), "XG%d" % (2 * tb + 1)])
                for half in range(2):
                    bank = next_main()
                    pv = psb(bank)
                    for kl in range(4):
                        kc = half * 4 + kl
                        S.op("pe", I("transpose", out=pv[:, kl * TB:(kl + 1) * TB],
                                     in_=XG[0:TB, 2 * tb + kc // 4, (kc % 4) * 128:(kc % 4) * 128 + 128],
                                     identity=identb[0:TB, 0:TB]),
                             reads=["XG%d" % (2 * tb + kc // 4), "identb"], writes=["ps%d" % bank], inc=(kl == 3))
                    gin = small[:, gcol + half * 4:gcol + half * 4 + 4].unsqueeze(2).to_broadcast([128, 4, TB])
                    S.op("dve", I("tensor_tensor", out=xT[:, half * 4:half * 4 + 4, tb * TB:(tb + 1) * TB],
                                  in0=pv[:, 0:4 * TB].rearrange("p (a b) -> p a b", b=TB), in1=gin, op=ALU.mult),
                         reads=["ps%d" % bank, "small"], writes=["xT%d" % kc for kc in range(half * 4, half * 4 + 4)])

        def fm2tm_out(srcs, n, res_reads, dst_fn, scale_ap=None):
            stag = SCR[4:6]
            for half in range(2):
                bank = next_main()
                for kl in range(4):
                    S.op("pe", I("transpose", out=PS[0:n, bank, kl * 128:(kl + 1) * 128], in_=srcs[half * 4 + kl],
                                 identity=identf[:]),
                         reads=list(res_reads) + ["identf"], writes=["ps%d" % bank], inc=(kl == 3))
                S.op("dve", I("tensor_copy", out=stag[half][0:n, :], in_=PS[0:n, bank, :]),
                     reads=["ps%d" % bank], writes=["SCR%d" % (4 + half)])
            dst_fn(stag)

        def build_ET(part):
            if part == "b":
                build_ET_b()
                return
            RB = SCR[0]
            OH = SCR[1]
            S.op("pool", I("memset", RB[0:64, 0:8], -30000.0), writes=["SCR0"])
            S.dma("sp", RB[0:32, 0:8], relb_d, writes=["SCR0"])
            S.dma("sp", OH[0:33, 0:384], oh_d, writes=["SCR1"])
            S.op("pe", I("matmul", PS[0:8, 0, 0:384], lhsT=RB[0:33, 0:8], rhs=OH[0:33, 0:384], start=True, stop=True),
                 reads=["SCR0", "SCR1"], writes=["ps0"])
            EG = SCR[2]
            S.op("act", I("activation", out=EG[0:8, 0:384], in_=PS[0:8, 0, 0:384], func=AF.Exp), reads=["ps0"], writes=["SCR2"])
            S.dma("sp", eg_scr, EG[0:8, 0:384], reads=["SCR2"], writes=["eg_scr"])
            EP = SH[:, 0:4, :]
            for blk in range(2):
                src = bass.AP(eg_scr.tensor, blk * 128, [[1, 128], [384, 8], [1, 128]])
                S.dma("sp", SH[:, 2 * blk:2 * blk + 2, :].rearrange("p a (b c) -> p (a b) c", c=128), src,
                      reads=["eg_scr"], writes=r_sh([2 * blk, 2 * blk + 1]))

        def build_ET_b():
            for kb in range(2):
                for qd in range(2):
                    bank = 1 + kb * 2 + qd
                    blk = 1 - kb
                    S.op("pe", I("matmul", PS[:, bank, :], lhsT=Jf[:], rhs=SH[:, 2 * blk + qd, :], start=True, stop=True),
                         reads=["Jf"] + r_sh([2 * blk + qd]), writes=["ps%d" % bank])
                    S.op("dve", I("tensor_copy", out=ET[:, kb, 4 * qd:4 * qd + 4, :],
                                  in_=PS[:, bank, :].rearrange("p (a b) -> p a b", b=128)),
                         reads=["ps%d" % bank], writes=["ET"])


        def do_tile(t):
            sample = (t == "s")
            N = NS_TOK if sample else 512
            NB = 1 if sample else 4
            TB = NS_TOK if sample else 128
            sh_unit = DEC_B if sample else 1
            HAL = 3 * sh_unit
            import os as _os2
            last_prompt = (t == NT - 1) and not int(_os2.environ.get("KT_NOLAST", "0"))
            xrv = (lambda c, a, b: xr[:, c, a:b])

            if sample:
                for tt in range(DEC_T):
                    S.dma("sp", x_tm[tt * DEC_B:(tt + 1) * DEC_B, 0, :], xs[:, tt, :], writes=["x0"])
                    S.dma("pool", p_bf[tt * DEC_B:(tt + 1) * DEC_B, 0, :], psm[:, tt, :], writes=["p_bf"])
            else:
                for tb in range(4):
                    r0 = t * 512 + tb * 128
                    if t > 0:
                        S.dma("sp", x_tm[:, tb, :], xp[r0:r0 + 128, :], writes=["x%d" % tb])
                S.dma("pool", p_bf[:], pp[t * 512:(t + 1) * 512, :].rearrange("(a p) n -> p a n", p=128), writes=["p_bf"])

            if t == 0:
                build_ET("a")
            rmsnorm_to_xT(N, NB, TB, C_G1, 0)
            XT_ALL = ["xT%d" % kc for kc in range(8)]

            deferred = []
            att_deferred = {}

            def tick(ci):
                while deferred and deferred[0][0] <= ci:
                    deferred.pop(0)[1]()

            for g in range(10):
                W, wres = get_piece("in%d" % g)
                nch = 4 if g < 9 else 2
                for j in range(nch):
                    ch = g * 4 + j
                    kind, cidx = CHUNKS[ch]
                    if kind == "v":
                        for tb in range(NB):
                            bank = next_main()
                            for kc in range(8):
                                S.op("pe", I("matmul", PS[0:TB, bank, 0:128], lhsT=xT[:, kc, tb * TB:(tb + 1) * TB],
                                             rhs=W[:, kc, j * 128:(j + 1) * 128], start=(kc == 0), stop=(kc == 7)),
                                     reads=XT_ALL + [wres], writes=["ps%d" % bank], inc=(kc == 7))
                            blk = tb + 1
                            if sample or (last_prompt and tb == 3):
                                S.op("dve", I("tensor_copy", out=SCR[3][0:TB, 0:128], in_=PS[0:TB, bank, 0:128]),
                                     reads=["ps%d" % bank], writes=["SCR3"])
                                S.op("act", I("activation", out=V[0:TB, blk, :, 0:64],
                                              in_=SCR[3][0:TB, 0:128].rearrange("p (a b) -> p a b", b=64), func=AF.Copy),
                                     reads=["SCR3"], writes=["V%d" % blk])
                            else:
                                S.op("act", I("activation", out=V[0:TB, blk, :, 0:64],
                                              in_=PS[0:TB, bank, 0:128].rearrange("p (a b) -> p a b", b=64), func=AF.Copy),
                                     reads=["ps%d" % bank], writes=["V%d" % blk])
                            if sample or (last_prompt and tb == 3):
                                if sample:
                                    for tt in range(DEC_T):
                                        S.dma("sp", vws[:, 124 + tt, :], SCR[3][tt * DEC_B:(tt + 1) * DEC_B, 0:128],
                                              reads=["SCR3"])
                                    S.dma("sp", vn_scr, SCR[3][0:NS_TOK, 0:128], reads=["SCR3"], writes=["vn_scr"])
                                else:
                                    S.dma("sp", vwp, SCR[3][:, 0:128], reads=["SCR3"])
                        continue
                    bank = next_main()
                    for kc in range(8):
                        S.op("pe", I("matmul", PS[:, bank, 0:N], lhsT=W[:, kc, j * 128:(j + 1) * 128], rhs=xT[:, kc, 0:N],
                                     start=(kc == 0), stop=(kc == 7)),
                             reads=XT_ALL + [wres], writes=["ps%d" % bank], inc=(kc == 7))
                    pr = "ps%d" % bank
                    pin = PS[:, bank, 0:N]
                    if kind == "xr":
                        rglru_part1(t, cidx, pin, pr, N, sample, sh_unit, HAL, last_prompt)
                    elif kind in ("q", "k"):
                        qk_chunk(t, (8 + cidx) if kind == "q" else 12, pin, pr, N, sample, last_prompt)
                    elif kind == "gr":
                        gr_chunk(cidx, pin, pr, N)
                    else:
                        which = 0 if kind == "ga" else 1
                        S.op("act", I("activation", out=SH[:, which * 8 + cidx, 0:N], in_=pin, func=AF.Tanh, scale=0.5),
                             reads=[pr], writes=r_sh([which * 8 + cidx]))
                    if t == 0 and kind == "grl" and cidx == 2:
                        build_ET("b")
                    ev = P2_SCHED.get((kind, cidx))
                    if ev is not None:
                        rglru_part2(t, ev[0], N, sample, last_prompt, phase=ev[1])
                    for fn_ in att_deferred.pop(ch, []):
                        fn_()
                if g == 5 and sample:
                    sample_hooks["attn"]()
                if g in ATT_AFTER_G and not sample:
                    tb = ATT_AFTER_G[g]
                    attention_block_a(t, tb)
                    due = min(g * 4 + 3 + 1, 37)
                    att_deferred.setdefault(due, []).append(lambda tb_=tb: attention_block_b(t, tb_))
                    due = min(g * 4 + 3 + 3, 38)
                    att_deferred.setdefault(due, []).append(lambda tb_=tb: attention_block_c(t, tb_))
            for due in sorted(att_deferred):
                for fn_ in att_deferred.pop(due):
                    fn_()
            if not sample:
                S.op("pool", I("tensor_copy", out=kT[:, 0:128], in_=kT[:, 512:640]), reads=["kT4"], writes=["kT0"])
                S.op("pool", I("tensor_copy", out=V[:, 0, :, 0:64], in_=V[:, 4, :, 0:64]), reads=["V4"], writes=["V0"])

            Woa, woa_r = get_piece("oa")
            Wor = [None, None]
            for oc in range(8):
                if oc % 4 == 0:
                    Wor[oc // 4] = get_piece("or%d" % (oc // 4))
                W2, w2r = Wor[oc // 4]
                bank = next_main()
                for kc in range(4):
                    S.op("pe", I("matmul", PS[:, bank, 0:N], lhsT=Woa[:, kc, oc * 128:(oc + 1) * 128], rhs=attT[:, kc, 0:N],
                                 start=(kc == 0), stop=(kc == 3)),
                         reads=["attT", woa_r], writes=["ps%d" % bank], inc=(kc == 3))
                S.op("dve", I("scalar_tensor_tensor", out=SCR[0][:, 0:N], in0=SH[:, oc, 0:N], scalar=1.0, in1=PS[:, bank, 0:N],
                              op0=ALU.add, op1=ALU.mult), reads=["ps%d" % bank] + r_sh([oc]), writes=["SCR0"])
                bank2 = next_main()
                for kc in range(8):
                    S.op("pe", I("matmul", PS[:, bank2, 0:N], lhsT=W2[:, kc, (oc % 4) * 128:(oc % 4 + 1) * 128], rhs=RIN[:, kc, 0:N],
                                 start=(kc == 0), stop=(kc == 7)),
                         reads=["RIN%d" % k for k in range(8)] + [w2r], writes=["ps%d" % bank2], inc=(kc == 7))
                S.op("dve", I("scalar_tensor_tensor", out=SCR[1][:, 0:N], in0=SH[:, 8 + oc, 0:N], scalar=1.0, in1=PS[:, bank2, 0:N],
                              op0=ALU.add, op1=ALU.mult), reads=["ps%d" % bank2] + r_sh([8 + oc]), writes=["SCR1"])
                S.op("pool", I("tensor_tensor", out=XG[:, oc, 0:N], in0=SCR[0][:, 0:N], in1=SCR[1][:, 0:N], op=ALU.add),
                     reads=["SCR0", "SCR1"], writes=["XG%d" % oc])

            XG_ALL = ["XG%d" % k for k in range(8)]
            for half in range(2):
                W, wres = get_piece("out%d" % half)
                for tb in range(NB):
                    bank = next_main()
                    for kc in range(8):
                        S.op("pe", I("matmul", PS[0:TB, bank, :], lhsT=XG[:, kc, tb * TB:(tb + 1) * TB], rhs=W[:, kc, :],
                                     start=(kc == 0), stop=(kc == 7)),
                             reads=XG_ALL + [wres], writes=["ps%d" % bank], inc=(kc == 7))
                    xs_ = x_tm[0:TB, tb, half * 512:(half + 1) * 512]
                    S.op("dve", I("scalar_tensor_tensor", out=xs_, in0=PS[0:TB, bank, :], scalar=0.5, in1=xs_,
                                  op0=ALU.mult, op1=ALU.add), reads=["ps%d" % bank, "x%d" % tb], writes=["x%d" % tb])

            if t == NT - 1 and "prep" in sample_hooks and _KT_SAMPLE:
                sample_hooks["prep"]()

            rmsnorm_to_xT(N, NB, TB, C_G2, 4)
            for g in range(8):
                W, wres = get_piece("up%d" % g)
                for j in range(4):
                    k = g * 4 + j
                    bank = next_main()
                    for kc in range(8):
                        S.op("pe", I("matmul", PS[:, bank, 0:N], lhsT=W[:, kc, j * 128:(j + 1) * 128], rhs=xT[:, kc, 0:N],
                                     start=(kc == 0), stop=(kc == 7)),
                             reads=XT_ALL + [wres], writes=["ps%d" % bank], inc=(kc == 7))
                    tb_ = SCR[2 + k % 2]
                    tr_ = "SCR%d" % (2 + k % 2)
                    S.op("act", I("activation", out=tb_[:, 0:N], in_=PS[:, bank, 0:N], func=AF.Relu),
                         reads=["ps%d" % bank], writes=[tr_])
                    S.op("pool",
                         I("tensor_tensor", out=hm(k, 0, N), in0=tb_[:, 0:N], in1=tb_[:, 0:N], op=ALU.mult),
                         reads=[tr_], writes=r_sh([k // 2]))
            for g in range(8):
                W, wres = get_piece("dn%d" % g)
                for tb in range(NB):
                    for half in range(2):
                        bank = tb * 2 + half
                        for kl in range(4):
                            k = g * 4 + kl
                            S.op("pe", I("matmul", PS[0:TB, bank, :], lhsT=hm(k, tb * TB, (tb + 1) * TB),
                                         rhs=W[:, kl, half * 512:(half + 1) * 512], start=(k == 0), stop=(k == 31)),
                                 reads=r_sh([k // 2]) + [wres], writes=["ps%d" % bank], inc=(kl == 3))
            for tb in range(NB):
                for half in range(2):
                    bank = tb * 2 + half
                    xs_ = x_tm[0:TB, tb, half * 512:(half + 1) * 512]
                    S.op("dve", I("tensor_tensor", out=xs_, in0=PS[0:TB, bank, :], in1=xs_, op=ALU.add),
                         reads=["ps%d" % bank, "x%d" % tb], writes=["x%d" % tb])

            rmsnorm_to_xT(N, NB, TB, C_G3, 8)
            for tb in range(NB):
                bank = next_main()
                pv = psb(bank)
                for kc2 in range(2):
                    S.op("pe", I("transpose", out=pv[:, kc2 * TB:(kc2 + 1) * TB], in_=p_bf[0:TB, tb, kc2 * 128:(kc2 + 1) * 128],
                                 identity=identb[0:TB, 0:TB]), reads=["p_bf", "identb"], writes=["ps%d" % bank], inc=(kc2 == 1))
                S.op("act", I("activation", out=pT[:, :, tb * TB:(tb + 1) * TB],
                              in_=pv[:, 0:2 * TB].rearrange("p (a b) -> p a b", b=TB), func=AF.Copy),
                     reads=["ps%d" % bank], writes=["pT"])
            for half in range(2):
                W, wres = get_piece("pg%d" % half)
                for tb in range(NB):
                    bank = next_main()
                    for kc in range(8):
                        S.op("pe", I("matmul", PS[0:TB, bank, :], lhsT=xT[:, kc, tb * TB:(tb + 1) * TB], rhs=W[:, kc, :],
                                     start=(kc == 0), stop=(kc == 7)),
                             reads=XT_ALL + [wres], writes=["ps%d" % bank], inc=(kc == 7))
                    bank2 = next_main()
                    for kc2 in range(2):
                        S.op("pe", I("matmul", PS[0:TB, bank2, :], lhsT=pT[:, kc2, tb * TB:(tb + 1) * TB],
                                     rhs=Wple_sb[:, kc2, half * 512:(half + 1) * 512], start=(kc2 == 0), stop=(kc2 == 1)),
                             reads=["pT", "Wple_sb"], writes=["ps%d" % bank2], inc=(kc2 == 1))
                    S.op("act", I("activation", out=SCR[0][0:TB, :], in_=PS[0:TB, bank, :], func=AF.Tanh, scale=0.5),
                         reads=["ps%d" % bank], writes=["SCR0"])
                    S.op("dve", I("scalar_tensor_tensor", out=SCR[1][0:TB, :], in0=SCR[0][0:TB, :], scalar=1.0, in1=PS[0:TB, bank2, :],
                                  op0=ALU.add, op1=ALU.mult), reads=["SCR0", "ps%d" % bank2], writes=["SCR1"])
                    xs_ = x_tm[0:TB, tb, half * 512:(half + 1) * 512]
                    S.op("dve", I("scalar_tensor_tensor", out=xs_, in0=SCR[1][0:TB, :], scalar=0.5, in1=xs_,
                                  op0=ALU.mult, op1=ALU.add), reads=["SCR1", "x%d" % tb], writes=["x%d" % tb])
                    if half == 1:
                        if sample:
                            for tt in range(DEC_T):
                                S.dma("sp", y_s[:, tt, :], x_tm[tt * DEC_B:(tt + 1) * DEC_B, 0, :], reads=["x0"])
                        else:
                            r0 = t * 512 + tb * 128
                            S.dma("sp", y_p[r0:r0 + 128, :], x_tm[:, tb, :], reads=["x%d" % tb])

        def rglru_part1(t, c, pin, pr, N, sample, su, HAL, last_prompt):
            r4 = c % 4
            XC = XCR[:, r4, :]
            xcn = "XC%d" % r4
            xcb = XCBR[:, r4, :]
            xb = xrS if sample else xr
            xrc = "xr%d" % c
            xrh = "xrh%d" % c
            S.op("act", I("activation", out=xb[:, c, HAL:HAL + N], in_=pin, func=AF.Copy), reads=[pr], writes=[xrc])
            cw = lambda j: small[:, C_CW + c * 4 + j:C_CW + c * 4 + j + 1]
            S.op("pool", I("tensor_scalar", out=XC[:, 0:N], in0=xb[:, c, 3 * su:3 * su + N], scalar1=cw(3),
                           scalar2=small[:, C_CB + c:C_CB + c + 1], op0=ALU.mult, op1=ALU.add),
                 reads=[xrc, xrh, "small"], writes=[xcn])
            for j in range(3):
                S.op("dve", I("scalar_tensor_tensor", out=XC[:, 0:N], in0=xb[:, c, j * su:j * su + N], scalar=cw(j), in1=XC[:, 0:N],
                              op0=ALU.mult, op1=ALU.add), reads=[xrc, xrh, "small", xcn], writes=[xcn])
            S.op("pool", I("tensor_copy", out=xcb[:, 0:N], in_=XC[:, 0:N]), reads=[xcn], writes=["xcb%d" % r4])
            if sample:
                fm_srcs_conv.append(xb[:, c, 4 * su:4 * su + 3 * su])
            else:
                if last_prompt:
                    S.op("pool", I("tensor_copy", out=convst[:, c, 0:3], in_=xb[:, c, N:N + 3]), reads=[xrc], writes=["convst"])
                else:
                    S.op("pool", I("tensor_copy", out=xb[:, c, 0:3], in_=xb[:, c, N:N + 3]), reads=[xrc, xcn], writes=[xrh])

        def rglru_part2(t, chunks, N, sample, last_prompt, phase="ab"):
            nj = len(chunks)
            TR = lambda j: P2[:, 0, j, :]
            TI = lambda j: P2[:, 1, j, :]
            A = lambda j: P2[:, 2, j, :]
            tn = lambda k, j: "P2_%d_%d" % (k, j)
            for j, c in (enumerate(chunks) if "a" in phase else []):
                r4 = c % 4
                xcb = XCBR[:, r4, :]
                b1 = 4 + 2 * j
                S.op("pe", I("matmul", PS[:, b1, 0:N], lhsT=rgA[:, c, :], rhs=xcb[:, 0:N], start=True, stop=True),
                     reads=["rgA", "xcb%d" % r4], writes=["ps%d" % b1])
                b2 = 5 + 2 * j
                S.op("pe", I("matmul", PS[:, b2, 0:N], lhsT=rgX[:, c, :], rhs=xcb[:, 0:N], start=True, stop=True),
                     reads=["rgX", "xcb%d" % r4], writes=["ps%d" % b2])
                S.op("act", I("activation", out=TR(j)[:, 0:N], in_=PS[:, b1, 0:N], func=AF.Tanh, scale=0.5,
                              bias=der[:, D_HBA + c:D_HBA + c + 1]), reads=["ps%d" % b1, "der_a"], writes=[tn(0, j)])
                S.op("act", I("activation", out=TI(j)[:, 0:N], in_=PS[:, b2, 0:N], func=AF.Tanh, scale=0.5,
                              bias=der[:, D_HBX + c:D_HBX + c + 1]), reads=["ps%d" % b2, "der_a"], writes=[tn(1, j)])
            for j, c in (enumerate(chunks) if "a" in phase else []):
                S.op("act", I("activation", out=A(j)[:, 0:N], in_=TR(j)[:, 0:N], func=AF.Exp, scale=der[:, D_HC + c:D_HC + c + 1],
                              bias=der[:, D_HC + c:D_HC + c + 1]), reads=[tn(0, j), "der_c2"], writes=[tn(2, j)])
                S.op("pool", I("tensor_tensor", out=TR(j)[:, 0:N], in0=A(j)[:, 0:N], in1=A(j)[:, 0:N], op=ALU.mult),
                     reads=[tn(2, j)], writes=[tn(0, j)])
                S.op("dve", I("scalar_tensor_tensor", out=TI(j)[:, 0:N], in0=TI(j)[:, 0:N], scalar=1.0, in1=XCR[:, c % 4, 0:N],
                              op0=ALU.add, op1=ALU.mult), reads=[tn(1, j), "XC%d" % (c % 4)], writes=[tn(1, j)])
            if "b" not in phase:
                return
            if nj == 2 and N == 512:
                trj = P2[:, 0, :, :].rearrange("p a n -> p (a n)")
                S.op("act", I("activation", out=trj, in_=trj, func=AF.Sqrt, scale=-1.0, bias=1.0),
                     reads=[tn(0, 0), tn(0, 1)], writes=[tn(0, 0), tn(0, 1)])
            else:
                for j in range(nj):
                    S.op("act", I("activation", out=TR(j)[:, 0:N], in_=TR(j)[:, 0:N], func=AF.Sqrt, scale=-1.0, bias=1.0),
                         reads=[tn(0, j)], writes=[tn(0, j)])
            for j, c in enumerate(chunks):
                r4 = c % 4
                XC = XCR[:, r4, :]
                xcn = "XC%d" % r4
                S.op("dve", I("scalar_tensor_tensor", out=TI(j)[:, 0:N], in0=TI(j)[:, 0:N], scalar=0.5, in1=TR(j)[:, 0:N],
                              op0=ALU.mult, op1=ALU.mult), reads=[tn(1, j), tn(0, j)], writes=[tn(1, j)])
                Hc = SH[:, c, :]
                if sample:
                    for tt in range(DEC_T):
                        prev = HS0[:, c, :] if tt == 0 else Hc[:, (tt - 1) * DEC_B:tt * DEC_B]
                        cur = Hc[:, tt * DEC_B:(tt + 1) * DEC_B]
                        S.op("dve", I("tensor_tensor", out=cur, in0=A(j)[:, tt * DEC_B:(tt + 1) * DEC_B], in1=prev, op=ALU.mult),
                             reads=[tn(2, j), "HS0"] + r_sh([c]), writes=r_sh([c]))
                        S.op("dve", I("tensor_tensor", out=cur, in0=cur, in1=TI(j)[:, tt * DEC_B:(tt + 1) * DEC_B], op=ALU.add),
                             reads=[tn(1, j)] + r_sh([c]), writes=r_sh([c]))
                    S.op("pool", I("tensor_copy", out=HSout[:, c, :], in_=Hc[:, 3 * DEC_B:4 * DEC_B]), reads=r_sh([c]), writes=["HSout"])
                    fm_srcs_h.append(HSout[:, c, :])
                else:
                    S.op("dve", I("tensor_tensor_scan", out=Hc[:, 0:N], data0=A(j)[:, 0:N], data1=TI(j)[:, 0:N],
                                  initial=hstate[:, c:c + 1], op0=ALU.mult, op1=ALU.add),
                         reads=[tn(2, j), tn(1, j), "hst%d" % c], writes=r_sh([c]))
                    S.op("pool", I("tensor_copy", out=hstate[:, c:c + 1], in_=Hc[:, N - 1:N]), reads=r_sh([c]), writes=["hst%d" % c])

        def gr_chunk(c, pin, pr, N):
            T1, T2 = SCR[2], SCR[3]
            S.op("act", I("activation", out=T1[:, 0:N], in_=pin, func=AF.Square), reads=[pr], writes=["SCR2"])
            S.op("pool", I("tensor_scalar", out=T1[:, 0:N], in0=T1[:, 0:N], scalar1=0.044715, scalar2=1.0, op0=ALU.mult, op1=ALU.add),
                 reads=["SCR2"], writes=["SCR2"])
            S.op("dve", I("tensor_tensor", out=T1[:, 0:N], in0=T1[:, 0:N], in1=pin, op=ALU.mult), reads=["SCR2", pr], writes=["SCR2"])
            S.op("act", I("activation", out=T2[:, 0:N], in_=T1[:, 0:N], func=AF.Tanh, scale=0.7978845608028654),
                 reads=["SCR2"], writes=["SCR3"])
            S.op("dve", I("scalar_tensor_tensor", out=T2[:, 0:N], in0=T2[:, 0:N], scalar=1.0, in1=pin, op0=ALU.add, op1=ALU.mult),
                 reads=["SCR3", pr], writes=["SCR3"])
            S.op("dve", I("scalar_tensor_tensor", out=RIN[:, c, 0:N], in0=T2[:, 0:N], scalar=0.5, in1=SH[:, c, 0:N],
                           op0=ALU.mult, op1=ALU.mult), reads=["SCR3"] + r_sh([c]), writes=["RIN%d" % c])

        def qk_chunk(t, ch, pin, pr, N, sample, last_prompt):
            isk = (ch == 12)
            sq = sqbuf
            S.op("act", I("activation", out=sq[:, 0:N], in_=pin, func=AF.Square), reads=[pr], writes=["sqbuf"])
            b2 = next_main()
            S.op("pe", I("matmul", PS[:, b2, 0:N], lhsT=onesblk[:], rhs=sq[:, 0:N], start=True, stop=True),
                 reads=["onesblk", "sqbuf"], writes=["ps%d" % b2])
            RS = SCR[2]
            S.op("act", I("activation", out=RS[:, 0:N], in_=PS[:, b2, 0:N], func=AF.Sqrt, bias=eps64[:, 0:1]),
                 reads=["ps%d" % b2, "eps64"], writes=["SCR2"])
            S.op("dve", I("reciprocal", out=RS[:, 0:N], in_=RS[:, 0:N]), reads=["SCR2"], writes=["SCR2"])
            if not isk:
                c = ch - 8
                S.op("dve", I("scalar_tensor_tensor", out=qT[:, c, 0:N], in0=pin, scalar=small[:, C_QG:C_QG + 1], in1=RS[:, 0:N],
                              op0=ALU.mult, op1=ALU.mult), reads=[pr, "SCR2", "small"], writes=["qT%d" % c])
            else:
                if sample:
                    dst = kTs[:, 0:N]
                    wr = ["kTs"]
                else:
                    dst = kT[:, 128:640]
                    wr = ["kT%d" % i for i in range(1, 5)]
                S.op("dve", I("scalar_tensor_tensor", out=dst, in0=pin, scalar=small[:, C_KG:C_KG + 1], in1=RS[:, 0:N],
                              op0=ALU.mult, op1=ALU.mult), reads=[pr, "SCR2", "small"], writes=wr)
                if sample or last_prompt:
                    n = NS_TOK if sample else 128
                    src_cols = slice(0, N) if sample else slice(384, 512)
                    S.op("dve", I("scalar_tensor_tensor", out=kTf[:, 0:n], in0=PS[:, int(pr[2:]), src_cols],
                                  scalar=der[:, D_KG8:D_KG8 + 1], in1=RS[:, src_cols], op0=ALU.mult, op1=ALU.mult),
                         reads=[pr, "SCR2", "der_k"], writes=["kTf"])
                    bank = next_main()
                    S.op("pe", I("transpose", out=PS[0:n, bank, 0:128], in_=kTf[:, 0:n], identity=identf[:]),
                         reads=["kTf", "identf"], writes=["ps%d" % bank])
                    S.op("act", I("activation", out=SCR[4][0:n, 0:128], in_=PS[0:n, bank, 0:128], func=AF.Copy),
                         reads=["ps%d" % bank], writes=["SCR4"])
                    if sample:
                        for tt in range(DEC_T):
                            S.dma("sp", kws[:, 124 + tt, :], SCR[4][tt * DEC_B:(tt + 1) * DEC_B, 0:128], reads=["SCR4"])
                    else:
                        S.dma("sp", kwp, SCR[4][:, 0:128], reads=["SCR4"])

        def attention_block_a(t, tb):
            gb = t * 4 + tb
            kbs = [0, 1] if gb > 0 else [1]
            for qd in range(2):
                hs_ = slice(64 * qd, 64 * qd + 64)
                for kb in kbs:
                    kcol = (tb + kb) * 128
                    bank = 4 + 2 * qd + kb
                    for c in range(4):
                        S.op("pe", I("matmul", PS[:, bank, c * 128:(c + 1) * 128], lhsT=kT[hs_, kcol:kcol + 128],
                                     rhs=qT[hs_, c, tb * 128:(tb + 1) * 128], start=True, stop=True),
                             reads=["kT%d" % (tb + kb), "qT%d" % c], writes=["ps%d" % bank], inc=(c == 3))
            for qd in range(2):
                E = SCR[4:6]
                pt = PT[qd]
                ptr = "PT%d" % qd
                for kb in kbs:
                    bank = 4 + 2 * qd + kb
                    S.op("act", I("activation", out=E[kb][:], in_=PS[:, bank, :], func=AF.Exp, scale=8.0),
                         reads=["ps%d" % bank], writes=["SCR%d" % (4 + kb)])
                    S.op("pool", I("tensor_tensor", out=pt[:, kb, :, :], in0=E[kb][:].rearrange("p (a b) -> p a b", b=128),
                                   in1=ET[:, kb, 4 * qd:4 * qd + 4, :], op=ALU.mult),
                         reads=["SCR%d" % (4 + kb), "ET"], writes=[ptr + "_%d" % kb])

        def attention_block_b(t, tb):
            gb = t * 4 + tb
            kbs = [0, 1] if gb > 0 else [1]
            at = att_tm[tb % 2]
            atr = "att_tm%d" % (tb % 2)
            for qd in range(2):
                pt = PT[qd]
                ptr = "PT%d" % qd
                ob = next_main()
                for c in range(4):
                    for i, kb in enumerate(kbs):
                        S.op("pe", I("matmul", PS[:, ob, c * 65:(c + 1) * 65], lhsT=pt[:, kb, c, :], rhs=V[:, tb + kb, qd, :],
                                     start=(i == 0), stop=(i == len(kbs) - 1)),
                             reads=[ptr + "_%d" % kb, "V%d" % (tb + kb)], writes=["ps%d" % ob], inc=(c == 3 and i == len(kbs) - 1))
                den = stats[:, 16 + 4 * qd:16 + 4 * qd + 4]
                po = PS[:, ob, 0:260].rearrange("p (a b) -> p a b", b=65)
                S.op("dve", I("tensor_tensor", out=den, in0=po[:, :, 64], in1=der[:, D_ES + 4 * qd:D_ES + 4 * qd + 4], op=ALU.add),
                     reads=["ps%d" % ob, "der_e"], writes=["den%d" % qd])
                S.op("dve", I("reciprocal", out=den, in_=den), reads=["den%d" % qd], writes=["den%d" % qd])
                S.op("dve", I("tensor_tensor", out=at[:, 256 * qd:256 * qd + 256].rearrange("p (a b) -> p a b", b=64),
                              in0=po[:, :, 0:64], in1=den.unsqueeze(2).to_broadcast([128, 4, 64]), op=ALU.mult),
                     reads=["ps%d" % ob, "den%d" % qd], writes=[atr])

        def attention_block_c(t, tb):
            at = att_tm[tb % 2]
            atr = "att_tm%d" % (tb % 2)
            bank = next_main()
            pv = psb(bank)
            for kc in range(4):
                S.op("pe", I("transpose", out=pv[:, kc * 128:(kc + 1) * 128], in_=at[:, kc * 128:(kc + 1) * 128], identity=identb[:]),
                     reads=[atr, "identb"], writes=["ps%d" % bank], inc=(kc == 3))
            S.op("act", I("activation", out=attT[:, :, tb * 128:(tb + 1) * 128],
                          in_=pv[:, 0:512].rearrange("p (a b) -> p a b", b=128), func=AF.Copy),
                 reads=["ps%d" % bank], writes=["attT"])

        def attention_sample():
            NQ = NS_TOK
            ckb, kTc, Vdc, Vdn = (sample_hooks[k] for k in ("ckb", "kTc", "Vdc", "Vdn"))
            for b4 in range(4):
                bank = next_main()
                pv = psb(bank)
                for bl in range(4):
                    b = b4 * 4 + bl
                    S.op("pe", I("transpose", out=pv[:, bl * 128:(bl + 1) * 128], in_=ckb[:, b, :], identity=identb[:]),
                         reads=["ckb", "identb"], writes=["ps%d" % bank], inc=(bl == 3))
                S.op("act", I("activation", out=kTc[:, b4 * 4:b4 * 4 + 4, :], in_=pv[:, 0:512].rearrange("p (a b) -> p a b", b=128),
                              func=AF.Copy), reads=["ps%d" % bank], writes=["kTc"])
            for b in range(DEC_B):
                for qd in range(2):
                    hs_ = slice(64 * qd, 64 * qd + 64)
                    qv = qT[hs_, :, 0:NQ].rearrange("p c (i b) -> p b c i", b=DEC_B)[:, b, :, :]
                    kv_new = kTs[hs_, 0:NQ].rearrange("p (i b) -> p b i", b=DEC_B)[:, b, :]
                    o = (b * 8 + 4 * qd) * 4
                    last = (b == DEC_B - 1 and qd == 1)
                    S.op("pe", I("matmul", PS[:, 4, o:o + 16].rearrange("p (c i) -> p c i", i=4), lhsT=kTc[hs_, b, :], rhs=qv,
                                 start=True, stop=True),
                         reads=["kTc"] + ["qT%d" % c for c in range(4)], writes=["ps4"], inc=False)
                    S.op("pe", I("matmul", PS[0:4, 5, o:o + 16].rearrange("p (c i) -> p c i", i=4), lhsT=kv_new, rhs=qv,
                                 start=True, stop=True),
                         reads=["kTs"] + ["qT%d" % c for c in range(4)], writes=["ps5"], inc=last)
            EA, EB = SCR[4], SCR[5]
            S.op("act", I("activation", out=EA[:], in_=PS[:, 4, :], func=AF.Exp, scale=1.0), reads=["ps4"], writes=["SCR4"])
            S.op("act", I("activation", out=EB[0:4, :], in_=PS[0:4, 5, :], func=AF.Exp, scale=8.0), reads=["ps5"], writes=["SCR5"])
            et0 = ET[:, 0, :, 0:4].unsqueeze(1).to_broadcast([128, DEC_B, 8, 4])
            et1 = ET[0:4, 1, :, 0:4].unsqueeze(1).to_broadcast([4, DEC_B, 8, 4])
            S.op("dve", I("tensor_tensor", out=PTA[:].rearrange("p (b h i) -> p b h i", h=8, i=4),
                          in0=EA[:].rearrange("p (b h i) -> p b h i", h=8, i=4), in1=et0, op=ALU.mult),
                 reads=["SCR4", "ET"], writes=["PTA"])
            S.op("dve", I("tensor_tensor", out=PTB[0:4, :].rearrange("p (b h i) -> p b h i", h=8, i=4),
                          in0=EB[0:4, :].rearrange("p (b h i) -> p b h i", h=8, i=4), in1=et1, op=ALU.mult),
                 reads=["SCR5", "ET"], writes=["PTB"])
            for b in range(DEC_B):
                for kvh in range(2):
                    o = (b * 8 + kvh * 4) * 4
                    S.op("pe", I("matmul", PS[:, 6, o:o + 16], lhsT=Vdc[:, b, kvh, :], rhs=PTA[:, o:o + 16], start=True, stop=False),
                         reads=["Vdc", "PTA"], writes=["ps6"], inc=False)
                    S.op("pe", I("matmul", PS[:, 6, o:o + 16], lhsT=Vdn[0:4, b, kvh, :], rhs=PTB[0:4, o:o + 16], start=False, stop=True),
                         reads=["Vdn", "PTB"], writes=["ps6"], inc=(b == DEC_B - 1 and kvh == 1))
            S.op("pe", I("matmul", PS[:, 7, :], lhsT=ones128[:], rhs=PTA[:], start=True, stop=False),
                 reads=["ones128", "PTA"], writes=["ps7"], inc=False)
            S.op("pe", I("matmul", PS[:, 7, :], lhsT=ones128[0:4, :], rhs=PTB[0:4, :], start=False, stop=True),
                 reads=["ones128", "PTB"], writes=["ps7"])
            DEN = SCR[4]
            S.op("dve", I("tensor_tensor", out=DEN[:], in0=PS[:, 7, :], in1=SINKC[:], op=ALU.add), reads=["ps7", "SINKC"], writes=["SCR4"])
            S.op("dve", I("reciprocal", out=DEN[:], in_=DEN[:]), reads=["SCR4"], writes=["SCR4"])
            for par in range(2):
                ps_ = slice(64 * par, 64 * par + 64)
                def view(ap_):
                    return ap_.rearrange("p (b kc two i) -> p kc two i b", kc=4, two=2, i=4)[:, :, par, :, :]
                S.op("dve", I("tensor_tensor", out=attT[ps_, :, 0:NQ].rearrange("p kc (i b) -> p kc i b", b=DEC_B),
                              in0=view(PS[ps_, 6, :]), in1=view(DEN[ps_, :]), op=ALU.mult),
                     reads=["ps6", "SCR4"], writes=["attT"])

        def sample_prep():
            XR_ALL = ["xr%d" % c_ for c_ in range(8)] + ["xrh%d" % c_ for c_ in range(8)]
            xr_b = xr_flat[:, 0:4096].bitcast(BF16)
            ckb = xr_b[:, 0:2048].rearrange("p (b k) -> p b k", k=128)
            kTc = xr_b[:, 2048:4096].rearrange("p (b k) -> p b k", k=128)
            Vdc = xr_b[:, 4096:8192].rearrange("p (b kvh e) -> p b kvh e", kvh=2, e=128)
            Vdn = x_tm[0:4, 1:3, :].rearrange("p a n -> p (a n)").bitcast(BF16).rearrange("p (b kvh e) -> p b kvh e", kvh=2, e=128)

            for b2 in range(2):
                bs = slice(b2 * 8, b2 * 8 + 8)
                S.dma("pool", ckb[:, bs, :], ck[bs].rearrange("b k d -> k b d"), writes=["ckb"] + XR_ALL)
            cvb = RIN[:, 0:4, :].rearrange("p a n -> p (a n)").rearrange("p (b k) -> p b k", k=128)
            for b2 in range(2):
                bs = slice(b2 * 8, b2 * 8 + 8)
                S.dma("pool", cvb[:, bs, :], cv[bs].rearrange("b k d -> k b d"), writes=["RIN%d" % k for k in range(4)])
            for dup in range(2):
                S.op("dve" if dup == 0 else "pool",
                     I("tensor_copy", out=Vdc[:, :, :, dup * 64:(dup + 1) * 64], in_=cvb.rearrange("p b (kvh d) -> p b kvh d", kvh=2)),
                     reads=["RIN%d" % k for k in range(4)], writes=["Vdc"] + XR_ALL)
            S.dma("sp", kws[:, 0:124, :], ck[:, 4:128, :])
            S.dma("sp", vws[:, 0:124, :], cv[:, 4:128, :])
            for j in range(3):
                S.dma("sp", SCR[0][j * DEC_B:(j + 1) * DEC_B, :], sconv[:, j, 0:512], writes=["SCR0"])
                S.dma("sp", SCR[1][j * DEC_B:(j + 1) * DEC_B, :], sconv[:, j, 512:1024], writes=["SCR1"])
            S.dma("sp", SCR[2][0:DEC_B, :], sh[:, 0:512], writes=["SCR2"])
            S.dma("sp", SCR[3][0:DEC_B, :], sh[:, 512:1024], writes=["SCR3"])
            for c in range(8):
                bank = next_main()
                S.op("pe", I("transpose", out=PS[:, bank, 0:48], in_=SCR[c // 4][0:48, (c % 4) * 128:(c % 4 + 1) * 128],
                             identity=identf[0:48, 0:48]), reads=["SCR%d" % (c // 4), "identf"], writes=["ps%d" % bank])
                S.op("act", I("activation", out=xrS[:, c, 0:48], in_=PS[:, bank, 0:48], func=AF.Copy),
                     reads=["ps%d" % bank], writes=["xrh%d" % c])
                bank = next_main()
                S.op("pe", I("transpose", out=PS[:, bank, 0:DEC_B], in_=SCR[2 + c // 4][0:DEC_B, (c % 4) * 128:(c % 4 + 1) * 128],
                             identity=identf[0:DEC_B, 0:DEC_B]), reads=["SCR%d" % (2 + c // 4), "identf"], writes=["ps%d" % bank])
                S.op("act", I("activation", out=HS0[:, c, :], in_=PS[:, bank, 0:DEC_B], func=AF.Copy),
                     reads=["ps%d" % bank], writes=["HS0"])
            S.op("dve", I("tensor_copy", out=SINKC[:].rearrange("p (b h i) -> p b h i", h=8, i=4),
                          in_=der[:, D_ES:D_ES + 8].unsqueeze(1).unsqueeze(3).to_broadcast([128, DEC_B, 8, 4])),
                 reads=["der_e"], writes=["SINKC"])

            sample_hooks.update(ckb=ckb, kTc=kTc, Vdc=Vdc, Vdn=Vdn)

        sample_hooks["prep"] = sample_prep

        S.dma("sp", Wple_sb[:], scr["ple"].rearrange("p (kc n) -> p kc n", n=1024),
              reads=["scrw_ple_0", "scrw_ple_1"], writes=["Wple_sb"])
        import os as _os
        _ntr = int(_os.environ.get("KT_TILES", str(NT)))
        _smp = int(_os.environ.get("KT_SAMPLE", "1"))
        for t in range(_ntr):
            do_tile(t)
            if t == NT - 1:
                bank = next_main()
                S.op("pe", I("transpose", out=PS[0:8, bank, 0:128], in_=hstate[:, 0:8], identity=identf[:]),
                     reads=["hst%d" % c for c in range(8)] + ["identf"], writes=["ps%d" % bank])
                S.op("pe", I("transpose", out=PS[0:32, bank, 128:256], in_=convst[:, :, :].rearrange("p c j -> p (c j)"),
                             identity=identf[:]), reads=["convst", "identf"], writes=["ps%d" % bank])
                S.op("dve", I("tensor_copy", out=SCR[4][0:32, 0:256], in_=PS[0:32, bank, 0:256]),
                     reads=["ps%d" % bank], writes=["SCR4"])
                S.dma("sp", hp.rearrange("o (c p) -> (o c) p", p=128), SCR[4][0:8, 0:128], reads=["SCR4"])
                for j in range(3):
                    S.dma("sp", convp[j:j + 1, :].rearrange("o (c p) -> (o c) p", p=128), SCR[4][j:32:4, 128:256], reads=["SCR4"])

        if not _smp:
            S.sync_all(["sp"])
            S.run(block, sems, dsems)
            return nc
        del fm_srcs_conv[:]
        del fm_srcs_h[:]
        orig_attention_sample = attention_sample

        def attention_sample_full():
            for dup in range(2):
                for kvh in range(2):
                    S.dma("pool", sample_hooks["Vdn"][:, :, kvh, dup * 64:(dup + 1) * 64],
                          vn_scr.rearrange("(t b) d -> t b d", b=DEC_B)[:, :, kvh * 64:(kvh + 1) * 64], reads=["vn_scr"],
                          writes=["Vdn", "x1", "x2"])
            orig_attention_sample()

        sample_hooks["attn"] = attention_sample_full
        do_tile("s")
        fm2tm_out(list(fm_srcs_conv), 48, ["xr%d" % c for c in range(8)],
                  lambda st: [S.dma("sp", convs[:, j, h2 * 512:(h2 + 1) * 512], st[h2][j * DEC_B:(j + 1) * DEC_B, :],
                                    reads=["SCR%d" % (4 + h2)]) for j in range(3) for h2 in range(2)])
        fm2tm_out(list(fm_srcs_h), DEC_B, ["HSout"],
                  lambda st: [S.dma("sp", hs[:, h2 * 512:(h2 + 1) * 512], st[h2][0:DEC_B, :], reads=["SCR%d" % (4 + h2)])
                              for h2 in range(2)])

        S.sync_all(["sp"])
        S.run(block, sems, dsems)
    return nc


_NC_CACHE = {}


def _host_layout(inp):
    f = lambda a: np.ascontiguousarray(np.asarray(a, dtype=np.float32))
    w_in = f(inp["w_in"])[0]
    q, k, v, xr_, gr, ga, grl = np.split(w_in, [512, 640, 768, 1792, 2816, 3840], axis=1)
    heads = [q[:, h * 64:(h + 1) * 64] for h in range(8)]
    qperm = np.concatenate([np.concatenate([heads[c], heads[c + 4]], axis=1) for c in range(4)], axis=1)
    src = {"xr": xr_, "grl": grl, "q": qperm, "k": k, "v": v, "gr": gr, "ga": ga}
    w_in_p = np.ascontiguousarray(np.concatenate([src[kd][:, ci * 128:(ci + 1) * 128] for kd, ci in CHUNKS], axis=1))

    def bd(w):
        o = np.zeros((128, 8, 128), np.float32)
        for c in range(8):
            o[0:64, c, 0:64] = w[2 * c]
            o[64:128, c, 64:128] = w[2 * c + 1]
        return o

    def pc(vv):
        return np.asarray(vv, np.float32).reshape(8, 128).T

    small = np.zeros((128, NSMALL), np.float32)
    small[:, C_G1:C_G1 + 8] = pc(inp["norm1_g"][0])
    small[:, C_G2:C_G2 + 8] = pc(inp["norm2_g"][0])
    small[:, C_G3:C_G3 + 8] = pc(inp["ple_norm_g"][0])
    cw = np.asarray(inp["conv_w"][0], np.float32)
    small[:, C_CW:C_CW + 32] = cw.reshape(4, 8, 128).transpose(2, 1, 0).reshape(128, 32)
    small[:, C_CB:C_CB + 8] = pc(inp["conv_b"][0])
    small[:, C_BA:C_BA + 8] = pc(inp["rg_ba"][0])
    small[:, C_BX:C_BX + 8] = pc(inp["rg_bx"][0])
    small[:, C_LAM:C_LAM + 8] = pc(inp["rg_lambda"][0])
    small[:, C_QG] = np.tile(np.asarray(inp["q_norm_g"][0], np.float32), 2)
    small[:, C_KG] = np.tile(np.asarray(inp["k_norm_g"][0], np.float32), 2)
    small[:, C_SINK:C_SINK + 8] = np.broadcast_to(np.asarray(inp["sinks"][0], np.float32)[None, :], (128, 8))

    oh = np.zeros((33, 384), np.float32)
    xs_ = np.arange(384)
    dist = xs_ - 127
    valid = (dist >= 0) & (dist <= 128)
    bucket = rel_bucket_np(dist)
    for x in range(384):
        if valid[x]:
            oh[bucket[x], x] = 1.0
        else:
            oh[32, x] = 1.0

    shared = {
        "w_in": w_in_p, "w_oa": f(inp["w_o_attn"])[0], "w_or": f(inp["w_o_rnn"])[0], "w_out": f(inp["w_out"])[0],
        "w_up": f(inp["w_up"])[0], "w_down": f(inp["w_down"])[0], "w_pg": f(inp["w_ple_gate"])[0], "w_ple": f(inp["w_ple"])[0],
        "rga": bd(np.asarray(inp["rg_wa"][0], np.float32)), "rgx": bd(np.asarray(inp["rg_wx"][0], np.float32)),
        "small": small, "relb": f(inp["rel_bias"]), "oh": oh,
    }
    maps = []
    for i in range(8):
        b0 = i * DEC_B
        m = dict(shared)
        m["xp"] = f(inp["x_prompt"][i])
        m["xs"] = f(inp["x_sample"][b0:b0 + DEC_B])
        m["pp"] = f(inp["p_prompt"][0, i])
        m["psm"] = f(inp["p_sample"][0, b0:b0 + DEC_B])
        m["ck"] = f(inp["cache_k_win"][0, b0:b0 + DEC_B]).reshape(DEC_B, 128, 128)
        m["cv"] = f(inp["cache_v_win"][0, b0:b0 + DEC_B]).reshape(DEC_B, 128, 128)
        m["sconv"] = f(inp["state_conv"][0, b0:b0 + DEC_B])
        m["sh"] = f(inp["state_h"][0, b0:b0 + DEC_B])
        maps.append(m)
    return maps


def kernel(**inputs):
    if "nc" not in _NC_CACHE:
        _NC_CACHE["nc"] = build_nc()
    nc = _NC_CACHE["nc"]
    maps = _host_layout(inputs)
    res = run_bass_kernel_spmd(nc, maps, core_ids=list(range(8)))
    R = res.results
    cat = lambda k: np.stack([np.asarray(r[k], np.float32) for r in R], axis=0)
    y_p = cat("y_p")
    y_s = cat("y_s").reshape(128, DEC_T, D)
    kwp = cat("kwp").reshape(1, 8, 128, 2, 64)
    vwp = cat("vwp").reshape(1, 8, 128, 2, 64)
    convp = cat("convp").reshape(1, 8, 3, D)
    hp = cat("hp").reshape(1, 8, D)
    kws = cat("kws").reshape(1, 128, 128, 2, 64)
    vws = cat("vws").reshape(1, 128, 128, 2, 64)
    convs = cat("convs").reshape(1, 128, 3, D)
    hs = cat("hs").reshape(1, 128, D)
    return (y_p, y_s, kwp, vwp, convp, hp, kws, vws, convs, hs)
```
